# Optimizing a Trainium2 kernel written in Bass

```python
import jax, jax.numpy as jnp
from jax import lax
import numpy as np

D_MODEL = 1024
BATCH = 16
SEQ = 2048
DEPTH = 2

CHUNK = 64
RW_HEAD_DIM = 64
RW_WIDTH = D_MODEL // 2
RW_HEADS = RW_WIDTH // RW_HEAD_DIM
DECAY_LORA = 64
AAA_LORA = 64
AT_HEAD_DIM = 64
AT_WIDTH = D_MODEL // 2
AT_HEADS = AT_WIDTH // AT_HEAD_DIM
LEFT_CHUNKS = 8
BAND_CHUNKS = LEFT_CHUNKS + 1
REL_CLIP = 2 * CHUNK
SG_CHUNK = 128
SG_WIDTH = D_MODEL
SG_GROUPS = 8
SG_GROUP_DIM = SG_WIDTH // SG_GROUPS
SHIFT_WIDTH = 3 * RW_WIDTH + DECAY_LORA + AAA_LORA
EVEN_IN = SHIFT_WIDTH + RW_WIDTH + 4 * AT_WIDTH
EVEN_MIX = RW_WIDTH + AT_WIDTH
ODD_IN = 3 * SG_WIDTH
RMS_EPS = 1e-6
LN_EPS = 1e-5
GN_EPS = 64e-5
NEG_INF = -1e30

kernel_name = "hybrid_rwkv7_chunkattn_gmlp_encoder"


def rmsnorm(x, g):
    x32 = x.astype(jnp.float32)
    y = x32 * lax.rsqrt(jnp.mean(x32 * x32, axis=-1, keepdims=True) + RMS_EPS)
    return (y * g.astype(jnp.float32)).astype(x.dtype)


def token_shift(p, mu):
    p_prev = jnp.pad(p, ((0, 0), (1, 0), (0, 0)))[:, :-1]
    return p + (p_prev - p) * mu


def rwkv7_mix(p_r, p_k, p_v, p_wd, p_ad, w0, w2, a0, a2, k_k, k_a, r_k, lnx_g, lnx_b):
    B, S, W = p_r.shape
    H, N = RW_HEADS, RW_HEAD_DIM
    f32 = jnp.float32
    r = p_r.astype(f32)
    k = p_k.astype(f32)
    v = p_v.astype(f32)
    w = -jax.nn.softplus(-(w0.astype(f32) + jnp.tanh(p_wd.astype(f32)) @ w2.astype(f32))) - 0.5
    decay = jnp.exp(-jnp.exp(w))
    a = jax.nn.sigmoid(a0.astype(f32) + p_ad.astype(f32) @ a2.astype(f32))
    heads = lambda t: t.reshape(B, S, H, N)
    kk = heads(k * k_k.astype(f32))
    kk = kk / jnp.maximum(jnp.sqrt(jnp.sum(kk * kk, axis=-1, keepdims=True)), 1e-12)
    k = k * (1.0 + (a - 1.0) * k_a.astype(f32))
    r_h, k_h, v_h, a_h = heads(r), heads(k), heads(v), heads(a)
    tm = lambda t: jnp.moveaxis(t, 1, 0)

    def step(state, inp):
        r_t, w_t, k_t, v_t, a_t, b_t = inp
        sa = jnp.einsum('bhvk,bhk->bhv', state, a_t)
        state = (state * w_t[:, :, None, :]
                 + sa[..., None] * b_t[:, :, None, :]
                 + v_t[..., None] * k_t[:, :, None, :])
        y_t = jnp.einsum('bhvk,bhk->bhv', state, r_t)
        return state, y_t

    s0 = jnp.zeros((B, H, N, N), f32)
    _, y = lax.scan(step, s0, (tm(r_h), tm(heads(decay)), tm(k_h), tm(v_h), tm(-kk), tm(kk * a_h)))
    y = jnp.moveaxis(y, 0, 1)
    mu = jnp.mean(y, axis=-1, keepdims=True)
    var = jnp.mean(jnp.square(y - mu), axis=-1, keepdims=True)
    y = ((y - mu) * lax.rsqrt(var + GN_EPS)).reshape(B, S, W)
    y = y * lnx_g.astype(f32) + lnx_b.astype(f32)
    bonus = jnp.sum(r_h * k_h * r_k.astype(f32), axis=-1, keepdims=True) * v_h
    y = y + bonus.reshape(B, S, W)
    return y.astype(p_r.dtype)


def chunk_attention(q, k, v, bias_table):
    B, S, W = q.shape
    H, Dh, L = AT_HEADS, AT_HEAD_DIM, CHUNK
    NC = S // L
    to_chunks = lambda t: t.reshape(B, NC, L, H, Dh).transpose(0, 3, 1, 2, 4)
    pad = ((0, 0), (0, 0), (LEFT_CHUNKS, 0), (0, 0), (0, 0))
    qc = to_chunks(q)
    kp = jnp.pad(to_chunks(k), pad)
    vp = jnp.pad(to_chunks(v), pad)
    qi = jnp.arange(L)
    kj = jnp.arange(BAND_CHUNKS * L)
    rel = LEFT_CHUNKS * L + qi[:, None] - kj[None, :]
    idx = jnp.clip(rel, -REL_CLIP, REL_CLIP) + REL_CLIP
    bias = bias_table[:, idx].astype(jnp.float32)
    scale = 1.0 / np.sqrt(Dh)

    def one_chunk(args):
        q_blk, c = args
        kb = lax.dynamic_slice_in_dim(kp, c, BAND_CHUNKS, axis=2).reshape(B, H, BAND_CHUNKS * L, Dh)
        vb = lax.dynamic_slice_in_dim(vp, c, BAND_CHUNKS, axis=2).reshape(B, H, BAND_CHUNKS * L, Dh)
        s = jnp.einsum('bhqd,bhkd->bhqk', q_blk, kb).astype(jnp.float32) * scale + bias
        valid = kj >= (LEFT_CHUNKS - c) * L
        s = jnp.where(valid[None, None, None, :], s, NEG_INF)
        p = jax.nn.softmax(s, axis=-1)
        return jnp.einsum('bhqk,bhkd->bhqd', p.astype(vb.dtype), vb)

    out = lax.map(one_chunk, (jnp.moveaxis(qc, 2, 0), jnp.arange(NC)))
    return out.transpose(1, 0, 3, 2, 4).reshape(B, S, W)


def spatial_gating(u, v, ln_g, ln_b, sg_w, sg_b):
    B, S, W = u.shape
    NB = S // SG_CHUNK
    v32 = v.astype(jnp.float32)
    mu = jnp.mean(v32, axis=-1, keepdims=True)
    var = jnp.mean(jnp.square(v32 - mu), axis=-1, keepdims=True)
    v = ((v32 - mu) * lax.rsqrt(var + LN_EPS) * ln_g.astype(jnp.float32)
         + ln_b.astype(jnp.float32)).astype(u.dtype)
    pos = jnp.arange(SG_CHUNK)
    mask = (pos[None, :] // CHUNK) <= (pos[:, None] // CHUNK)
    w = sg_w * mask[None].astype(sg_w.dtype)
    vb = v.reshape(B, NB, SG_CHUNK, SG_GROUPS, SG_GROUP_DIM)
    sv = jnp.einsum('gij,bnjgc->bnigc', w, vb) + sg_b.T[None, None, :, :, None]
    return u * sv.reshape(B, S, W)


def even_layer(h, g, w_in, shift_mu, w0, w2, a0, a2, k_k, k_a, r_k, lnx_g, lnx_b, att_bias, w_out):
    p = rmsnorm(h, g) @ w_in
    p_shift = token_shift(p[..., :SHIFT_WIDTH], shift_mu)
    p_r, p_k, p_v, p_wd, p_ad = jnp.split(
        p_shift, [RW_WIDTH, 2 * RW_WIDTH, 3 * RW_WIDTH, 3 * RW_WIDTH + DECAY_LORA], axis=-1)
    rest = p[..., SHIFT_WIDTH:]
    gate_a, q_b, k_b, v_b, gate_b = jnp.split(
        rest, [RW_WIDTH, RW_WIDTH + AT_WIDTH, RW_WIDTH + 2 * AT_WIDTH, RW_WIDTH + 3 * AT_WIDTH], axis=-1)
    y_a = rwkv7_mix(p_r, p_k, p_v, p_wd, p_ad, w0, w2, a0, a2, k_k, k_a, r_k, lnx_g, lnx_b)
    y_a = y_a * jax.nn.silu(gate_a)
    y_b = chunk_attention(q_b, k_b, v_b, att_bias) * jax.nn.silu(gate_b)
    return h + jnp.concatenate([y_a, y_b], axis=-1) @ w_out


def odd_layer(h, g, w_in, ln_g, ln_b, sg_w, sg_b, w_out):
    p = rmsnorm(h, g) @ w_in
    u, v, gate = jnp.split(p, [SG_WIDTH, 2 * SG_WIDTH], axis=-1)
    y = spatial_gating(jax.nn.gelu(u), jax.nn.gelu(v), ln_g, ln_b, sg_w, sg_b)
    return h + (y * jax.nn.silu(gate)) @ w_out


def setup_inputs(seed: int = 0) -> dict:
    key = jax.random.key(seed)
    ks = jax.random.split(key, 24)
    ne = (DEPTH + 1) // 2
    no = DEPTH // 2
    nrm = lambda k, shape, s: jax.random.normal(k, shape, jnp.float32) * s
    return {
        "x": nrm(ks[0], (BATCH, SEQ, D_MODEL), 1.0),
        "norm_g": 1.0 + nrm(ks[1], (DEPTH, D_MODEL), 0.01),
        "w_in_e": nrm(ks[2], (ne, D_MODEL, EVEN_IN), D_MODEL ** -0.5),
        "shift_mu": jax.random.uniform(ks[3], (ne, SHIFT_WIDTH), jnp.float32),
        "rw_w0": jax.random.uniform(ks[4], (ne, RW_WIDTH), jnp.float32, -4.0, 1.0),
        "rw_w2": nrm(ks[5], (ne, DECAY_LORA, RW_WIDTH), 0.5 * DECAY_LORA ** -0.5),
        "rw_a0": nrm(ks[6], (ne, RW_WIDTH), 0.1),
        "rw_a2": nrm(ks[7], (ne, AAA_LORA, RW_WIDTH), 0.5 * AAA_LORA ** -0.5),
        "rw_kk": 0.85 + nrm(ks[8], (ne, RW_WIDTH), 0.02),
        "rw_ka": 1.0 + nrm(ks[9], (ne, RW_WIDTH), 0.02),
        "rw_rk": nrm(ks[10], (ne, RW_HEADS, RW_HEAD_DIM), 0.1),
        "rw_lnx_g": 1.0 + nrm(ks[11], (ne, RW_WIDTH), 0.01),
        "rw_lnx_b": nrm(ks[12], (ne, RW_WIDTH), 0.01),
        "att_bias": nrm(ks[13], (ne, AT_HEADS, 2 * REL_CLIP + 1), 0.1),
        "w_out_e": nrm(ks[14], (ne, EVEN_MIX, D_MODEL), 0.5 * EVEN_MIX ** -0.5),
        "w_in_o": nrm(ks[15], (no, D_MODEL, ODD_IN), D_MODEL ** -0.5),
        "sg_ln_g": 1.0 + nrm(ks[16], (no, SG_WIDTH), 0.01),
        "sg_ln_b": nrm(ks[17], (no, SG_WIDTH), 0.01),
        "sg_w": nrm(ks[18], (no, SG_GROUPS, SG_CHUNK, SG_CHUNK), SG_CHUNK ** -0.5),
        "sg_b": 1.0 + nrm(ks[19], (no, SG_GROUPS, SG_CHUNK), 0.01),
        "w_out_o": nrm(ks[20], (no, SG_WIDTH, D_MODEL), 0.5 * SG_WIDTH ** -0.5),
        "final_g": 1.0 + nrm(ks[21], (D_MODEL,), 0.01),
    }


def reference(x, norm_g, w_in_e, shift_mu, rw_w0, rw_w2, rw_a0, rw_a2, rw_kk, rw_ka, rw_rk,
              rw_lnx_g, rw_lnx_b, att_bias, w_out_e, w_in_o, sg_ln_g, sg_ln_b, sg_w, sg_b,
              w_out_o, final_g):
    h = x
    for layer in range(DEPTH):
        i = layer // 2
        if layer % 2 == 0:
            h = even_layer(h, norm_g[layer], w_in_e[i], shift_mu[i], rw_w0[i], rw_w2[i], rw_a0[i],
                           rw_a2[i], rw_kk[i], rw_ka[i], rw_rk[i], rw_lnx_g[i], rw_lnx_b[i],
                           att_bias[i], w_out_e[i])
        else:
            h = odd_layer(h, norm_g[layer], w_in_o[i], sg_ln_g[i], sg_ln_b[i], sg_w[i], sg_b[i],
                          w_out_o[i])
    return rmsnorm(h, final_g)
```

```python
import contextlib
import numpy as np
import concourse.bass as bass
import concourse.mybir as mybir
from concourse.bass_utils import run_bass_kernel_spmd

F32 = mybir.dt.float32
BF16 = mybir.dt.bfloat16
AF = mybir.ActivationFunctionType
ALU = mybir.AluOpType
AX = mybir.AxisListType

NCORES = 8
D = 1024
SEQ = 2048
NB = 2
NTOK = NB * SEQ
TT = 512
EPOCH = 12000


class Buf:
    __slots__ = ("name", "w", "r")

    def __init__(self, name):
        self.name = name
        self.w = None
        self.r = {}


class Op:
    __slots__ = ("eng", "fn", "deps", "pos", "signal", "count", "epoch", "dma", "sem_idx",
                 "target", "prev_target", "waits", "tag", "seq")

    def __init__(self, eng, fn, dma):
        self.eng = eng
        self.fn = fn
        self.dma = dma
        self.deps = ()
        self.signal = False
        self.count = 0
        self.epoch = 0
        self.waits = ()
        self.tag = None


class Sched:
    ENGS = ("pe", "act", "dve", "pool", "sp")

    def __init__(self, nc, es, n_dma_sems=40, n_epochs=6):
        self.nc = nc
        self.engs = {"pe": nc.tensor, "act": nc.scalar, "dve": nc.vector, "pool": nc.gpsimd,
                     "sp": nc.sync}
        self.ops = {e: [] for e in self.ENGS}
        self.sems = {e: [es.enter_context(nc.semaphore(f"s_{e}{k}")) for k in range(n_epochs)]
                     for e in self.ENGS if e != "sp"}
        self.dma_sems = [es.enter_context(nc.semaphore(f"s_dma{k}")) for k in range(n_dma_sems)]
        self.dma_targets = [0] * n_dma_sems
        self.dma_rr = 0
        self.nbuf = 0

    def buf(self, name=None):
        self.nbuf += 1
        return Buf(name or f"b{self.nbuf}")

    def bufs(self, n, name="b"):
        return [self.buf(f"{name}{i}") for i in range(n)]

    def add(self, eng, fn, r=(), w=(), dma=False):
        op = Op(eng, fn, dma)
        deps = set()
        for b in r:
            if b.w is not None:
                deps.add(b.w)
        for b in w:
            if b.w is not None:
                deps.add(b.w)
            for o in b.r.values():
                deps.add(o)
        op.deps = deps
        for b in w:
            b.w = op
            b.r = {}
        key = id(op) if dma else eng
        for b in r:
            b.r[key] = op
        lst = self.ops[eng]
        op.pos = len(lst)
        op.seq = self.nseq = getattr(self, "nseq", 0) + 1
        lst.append(op)
        if dma:
            fns = fn if isinstance(fn, (list, tuple)) else [fn]
            op.fn = fns
            nsw = 10
            if eng == "pool":
                k = self.dma_rr_sw = (getattr(self, "dma_rr_sw", -1) + 1) % nsw
            else:
                k = nsw + self.dma_rr
                self.dma_rr = (self.dma_rr + 1) % (len(self.dma_sems) - nsw)
            op.sem_idx = k
            op.prev_target = self.dma_targets[k]
            self.dma_targets[k] += 16 * len(fns)
            op.target = self.dma_targets[k]
        return op

    def pe(self, fn, r=(), w=()):
        return self.add("pe", fn, r, w)

    def mm(self, out, lhsT, rhs, start=True, stop=True, r=(), w=(), skip=False):
        if skip:
            op = self.add("pe", lambda e: e.matmul(out, lhsT, rhs, start=start, stop=stop, skip_group_check=True), r, w)
        else:
            op = self.add("pe", lambda e: e.matmul(out, lhsT, rhs, start=start, stop=stop), r, w)
        la = lhsT.ap
        k = la[0][1]
        if k <= 64:
            rg = (lhsT.offset // la[0][0]) // 64
            oa = out.ap
            esz = 2 if out.dtype == BF16 else 4
            bank = (out.tensor.name, ((out.offset % oa[0][0]) * esz) // 2048)
            op.tag = (bank, rg)
        return op

    def act(self, fn, r=(), w=()):
        return self.add("act", fn, r, w)

    def dve(self, fn, r=(), w=()):
        return self.add("dve", fn, r, w)

    def pool(self, fn, r=(), w=()):
        return self.add("pool", fn, r, w)

    def dma(self, eng, fn, r=(), w=()):
        op = self.add(eng, fn, r, w, dma=True)
        self.open_dmas = getattr(self, "open_dmas", []) + [op]
        return op

    def barrier(self):
        deps = set(getattr(self, "open_dmas", []))
        self.open_dmas = []
        for eng in self.ENGS:
            if self.ops[eng]:
                deps.add(self.ops[eng][-1])
        for eng in self.ENGS:
            op = Op(eng, None, False)
            op.deps = set(deps)
            lst = self.ops[eng]
            op.pos = len(lst)
            op.seq = self.nseq = getattr(self, "nseq", 0) + 1
            lst.append(op)

    def emit(self, final_ops=(), maxops=None):
        if maxops is not None:
            for eng in self.ENGS:
                self.ops[eng] = [o for o in self.ops[eng] if o.seq <= maxops]
            final_ops = [o for o in final_ops if o.seq <= maxops]
        if not hasattr(self, "_st"):
            self._st = {e: {"wm": {}, "wmd": {}, "lastrow": {}, "c": 0, "ep": 0, "done": 0} for e in self.ENGS}
            self.stats = {"waits": 0}
        for eng in self.ENGS:
            st = self._st[eng]
            wm, wmd, lastrow = st["wm"], st["wmd"], st["lastrow"]
            for op in self.ops[eng][st["done"]:]:
                best = {}
                if eng == "pe" and op.tag is not None:
                    bank, rg = op.tag
                    for rg2, d in lastrow.items():
                        if rg2 != rg and wm.get("pe", -1) < d.pos:
                            if "pe" not in best or best["pe"].pos < d.pos:
                                best["pe"] = d
                    lastrow[rg] = op
                for d in op.deps:
                    if d.dma:
                        if wmd.get(d.sem_idx, 0) >= d.target:
                            continue
                        key = ("dma", d.sem_idx)
                        if key not in best or best[key].target < d.target:
                            best[key] = d
                    else:
                        if d.fn is None:
                            continue
                        if d.eng == "pe" and eng == "pe" and not op.dma:
                            continue
                        if wm.get(d.eng, -1) >= d.pos:
                            continue
                        if d.eng not in best or best[d.eng].pos < d.pos:
                            best[d.eng] = d
                waits = []
                for d in best.values():
                    if d.dma:
                        wmd[d.sem_idx] = d.target
                    else:
                        wm[d.eng] = d.pos
                        d.signal = True
                    waits.append(d)
                if op.dma and op.prev_target > 0 and wmd.get(op.sem_idx, 0) < op.prev_target:
                    wmd[op.sem_idx] = op.prev_target
                    waits.append(("reuse", op.sem_idx, op.prev_target))
                op.waits = waits
        for eng in self.ENGS:
            st = self._st[eng]
            c, ep = st["c"], st["ep"]
            for op in self.ops[eng][st["done"]:]:
                if op.signal and not op.dma:
                    c += 1
                    if c > EPOCH:
                        ep += 1
                        c = 1
                    op.count = c
                    op.epoch = ep
            st["c"], st["ep"] = c, ep
        nwait = 0
        for eng in self.ENGS:
            e = self.engs[eng]
            st = self._st[eng]
            for op in self.ops[eng][st["done"]:]:
                for d in op.waits:
                    nwait += 1
                    if isinstance(d, tuple):
                        e.wait_ge(self.dma_sems[d[1]], d[2])
                    elif d.dma:
                        e.wait_ge(self.dma_sems[d.sem_idx], d.target)
                    else:
                        e.wait_ge(self.sems[d.eng][d.epoch], d.count)
                if op.dma:
                    for f in op.fn:
                        f(e).then_inc(self.dma_sems[op.sem_idx], 16)
                else:
                    if op.fn is None:
                        continue
                    ins = op.fn(e)
                    if op.signal:
                        ins.then_inc(self.sems[eng][op.epoch], 1)
            st["done"] = len(self.ops[eng])
        e = self.engs["sp"]
        for op in final_ops:
            e.wait_ge(self.dma_sems[op.sem_idx], op.target)
        for k, v in self.ops.items():
            self.stats[k] = len(v)
        self.stats["waits"] += nwait


class PsumPool:
    def __init__(self, S, nc, es, n, name="ps"):
        self.tiles = [es.enter_context(nc.psum_tensor(f"{name}{i}", [128, 512], F32)) for i in range(n)]
        self.bufs = [S.buf(f"{name}{i}") for i in range(n)]
        self.i = 0

    def next(self):
        k = self.i
        self.i = (k + 1) % len(self.tiles)
        return self.tiles[k], self.bufs[k]


class Rot:
    def __init__(self, S, nc, es, n, shape, dtype, name):
        self.tiles = [es.enter_context(nc.sbuf_tensor(f"{name}{i}", shape, dtype)) for i in range(n)]
        self.bufs = [S.buf(f"{name}{i}") for i in range(n)]
        self.i = 0

    def next(self):
        k = self.i
        self.i = (k + 1) % len(self.tiles)
        return self.tiles[k], self.bufs[k]


def load_weight_bf16(S, w_sb, w_buf, w_dram, nk, ncols, col0=0, colstep=2048):
    fns = []
    for k in range(nk):
        for c0 in range(0, ncols, colstep):
            c1 = min(ncols, c0 + colstep)
            fns.append(lambda e, k=k, c0=c0, c1=c1: e.dma_start(
                out=w_sb[:, k, c0:c1], in_=w_dram[k * 128:(k + 1) * 128, col0 + c0:col0 + c1]))
    return S.dma("pool", fns, w=[w_buf])


def rstd_from_sumsq(S, ss, ss_buf, out, out_buf, tmp, tmp_buf, inv_n, eps):
    S.act(lambda e: e.activation(out=tmp, in_=ss, func=AF.Ln, bias=float(eps), scale=float(inv_n)),
          r=[ss_buf], w=[tmp_buf])
    S.act(lambda e: e.activation(out=out, in_=tmp, func=AF.Exp, scale=-0.5), r=[tmp_buf], w=[out_buf])


def phase2(S, nc, es0, dr, ntiles):
    es = contextlib.ExitStack()
    es0.enter_context(es)
    sb = lambda name, shape, dt=F32: es.enter_context(nc.sbuf_tensor(name, shape, dt))
    h1 = dr["h1"]
    out = dr["out"]

    w1 = sb("p2_w1", [128, 8, 3072], BF16); w1_b = S.buf("w1")
    w2 = sb("p2_w2", [128, 8, 1024], BF16); w2_b = S.buf("w2")
    ident = sb("p2_ident", [128, 128], BF16); ident_b = S.buf("ident")
    g1b = sb("p2_g1b", [128, 1024]); g1b_b = S.buf()
    fgb = sb("p2_fgb", [128, 1024]); fgb_b = S.buf()
    lng = sb("p2_lng", [128, 8]); lng_b = S.buf()
    lnb = sb("p2_lnb", [128, 8]); lnb_b = S.buf()
    WT = sb("p2_WT", [128, 8, 128], BF16); WT_b = S.buf()
    sgbb = sb("p2_sgbb", [128, 8, 128]); sgbb_b = S.buf()
    Bsv = sb("p2_Bsv", [128, 8, 128]); Bsv_b = S.buf()
    ones = sb("p2_ones", [128, 128], BF16); ones_b = S.buf()

    S.dma("pool", lambda e: e.dma_start(out=ident[:], in_=dr["ident"]), w=[ident_b])
    S.dma("sp", lambda e: e.dma_start(out=g1b[:], in_=dr["norm_g"][1:2, :].to_broadcast([128, D])), w=[g1b_b])
    S.dma("sp", lambda e: e.dma_start(out=fgb[:], in_=dr["final_g"][0:1, :].to_broadcast([128, D])), w=[fgb_b])
    S.dma("sp", lambda e: e.dma_start(out=lng[:], in_=dr["sg_ln_g_c"]), w=[lng_b])
    S.dma("sp", lambda e: e.dma_start(out=lnb[:], in_=dr["sg_ln_b_c"]), w=[lnb_b])
    S.dma("sp", lambda e: e.dma_start(out=sgbb[:].rearrange("p g i -> p (g i)"),
                                      in_=dr["sg_b"][0:1, :].to_broadcast([128, 1024])), w=[sgbb_b])
    S.dma("pool", [lambda e, g=g: e.dma_start(out=WT[:, g, :], in_=dr["sg_wT"][g]) for g in range(8)], w=[WT_b])
    load_weight_bf16(S, w1, w1_b, dr["w_in_o"], 8, 3072, colstep=1024)
    load_weight_bf16(S, w2, w2_b, dr["w_out_o"], 8, 1024)
    S.pool(lambda e: e.memset(WT[64:128, :, 0:64], 0.0), w=[WT_b])
    S.pool(lambda e: e.memset(ones[:], 1.0), w=[ones_b])

    pp = PsumPool(S, nc, es, 7, "p2ps")
    pst = es.enter_context(nc.psum_tensor("p2pst", [128, 8, 128], BF16)); pst_b = S.buf("pst")

    for g in range(8):
        ps, ps_b = pp.next()
        S.pe(lambda e, g=g, ps=ps: e.matmul(ps[:, 0:128], ones[:], WT[:, g, :], start=True, stop=True),
             r=[ones_b, WT_b], w=[ps_b])
        S.dve(lambda e, g=g, ps=ps: e.scalar_tensor_tensor(out=Bsv[:, g, :], in0=ps[:, 0:128], scalar=lnb[:, g:g + 1],
                                                          in1=sgbb[:, g, :], op0=ALU.mult, op1=ALU.add),
              r=[ps_b, lnb_b, sgbb_b], w=[Bsv_b])

    hin = sb("p2_hin", [128, 4, 1024]); hin_b = [S.buf(f"hin{s}") for s in range(4)]
    xn = Rot(S, nc, es, 2, [128, 1024], BF16, "p2_xn")
    sq = sb("p2_sq", [128, 1024], BF16); sq_b = S.buf("sq")
    xnT = sb("p2_xnT", [128, 8, 512], BF16); xnT_b = [S.buf(f"xnT{s}") for s in range(4)]
    gvbig = sb("p2_gvbig", [128, 4, 1024]); gvbig_b = [S.buf(f"gv{s}") for s in range(4)]
    vhat = sb("p2_vhat", [128, 4, 1024], BF16); vhat_b = [S.buf(f"vhat{s}") for s in range(4)]
    t1 = Rot(S, nc, es, 2, [128, 512], F32, "p2_t1")
    t2 = Rot(S, nc, es, 2, [128, 512], F32, "p2_t2")
    t3 = Rot(S, nc, es, 2, [128, 512], F32, "p2_t3")
    t4 = Rot(S, nc, es, 2, [128, 512], F32, "p2_t4")
    yT = sb("p2_yT", [128, 8, 512], BF16); yT_b = [S.buf(f"yT{g}") for g in range(8)]
    ot = Rot(S, nc, es, 2, [128, 1024], F32, "p2_ot")
    ss = sb("p2_ss", [128, 4]); ss_b = S.buf("ss")
    lt = sb("p2_lt", [128, 4]); lt_b = S.buf("lt")
    rs = sb("p2_rs", [128, 4]); rs_b = S.buf("rs")
    bst = sb("p2_bst", [128, 4, 2, 6]); bst_b = S.buf("bst")
    mv = sb("p2_mv", [128, 4, 2]); mv_b = S.buf("mv")
    vr = sb("p2_vr", [128, 4]); vr_b = S.buf("vr")
    ss2 = sb("p2_ss2", [128, 4]); ss2_b = S.buf("ss2")
    rs2 = sb("p2_rs2", [128, 4]); rs2_b = S.buf("rs2")

    out_ops = []
    for tt in range(ntiles):
        r0 = tt * TT
        for s in range(4):
            S.dma("sp", lambda e, s=s, r0=r0: e.dma_start(out=hin[:, s, :], in_=h1[r0 + s * 128:r0 + (s + 1) * 128, :]),
                  w=[hin_b[s]])
        S.pool(lambda e: e.memset(ss[:], 0.0), w=[ss_b])
        for s in range(4):
            S.act(lambda e, s=s: e.activation(out=sq[:], in_=hin[:, s, :], func=AF.Square, accum_out=ss[:, s:s + 1]),
                  r=[hin_b[s]], w=[sq_b, ss_b])
        rstd_from_sumsq(S, ss[:], ss_b, rs[:], rs_b, lt[:], lt_b, 1.0 / D, 1e-6)
        for s in range(4):
            xt, xt_b = xn.next()
            S.dve(lambda e, s=s, xt=xt: e.scalar_tensor_tensor(out=xt[:], in0=hin[:, s, :], scalar=rs[:, s:s + 1],
                                                             in1=g1b[:], op0=ALU.mult, op1=ALU.mult),
                  r=[hin_b[s], rs_b, g1b_b], w=[xt_b])
            for k in range(8):
                S.pe(lambda e, k=k, xt=xt: e.transpose(pst[:, k, :], xt[:, k * 128:(k + 1) * 128], ident[:]),
                     r=[xt_b, ident_b], w=[pst_b])
            S.act(lambda e, s=s: e.copy(out=xnT[:, :, s * 128:(s + 1) * 128], in_=pst[:]), r=[pst_b], w=[xnT_b[s]])
        for s in range(4):
            for hh in range(2):
                ps, ps_b = pp.next()
                for k in range(8):
                    S.pe(lambda e, s=s, hh=hh, k=k, ps=ps: e.matmul(
                        ps[:], xnT[:, k, s * 128:(s + 1) * 128], w1[:, k, 1024 + hh * 512:1024 + (hh + 1) * 512],
                        start=(k == 0), stop=(k == 7)), r=[xnT_b[s], w1_b], w=[ps_b])
                S.act(lambda e, s=s, hh=hh, ps=ps: e.activation(out=gvbig[:, s, hh * 512:(hh + 1) * 512], in_=ps[:],
                                                               func=AF.Gelu_apprx_tanh), r=[ps_b], w=[gvbig_b[s]])
            for hh in range(2):
                S.dve(lambda e, s=s, hh=hh: e.bn_stats(out=bst[:, s, hh, :], in_=gvbig[:, s, hh * 512:(hh + 1) * 512]),
                      r=[gvbig_b[s]], w=[bst_b])
            S.dve(lambda e, s=s: e.bn_aggr(out=mv[:, s, :], in_=bst[:, s, :, :].rearrange("p a b -> p (a b)")), r=[bst_b], w=[mv_b])
        S.act(lambda e: e.activation(out=lt[:], in_=mv[:, :, 1], func=AF.Ln, bias=1e-5, scale=1.0), r=[mv_b], w=[lt_b])
        S.act(lambda e: e.activation(out=vr[:], in_=lt[:], func=AF.Exp, scale=-0.5), r=[lt_b], w=[vr_b])
        for s in range(4):
            S.dve(lambda e, s=s: e.tensor_scalar(out=vhat[:, s, :], in0=gvbig[:, s, :], scalar1=mv[:, s, 0:1],
                                                scalar2=vr[:, s:s + 1], op0=ALU.subtract, op1=ALU.mult),
                  r=[gvbig_b[s], mv_b, vr_b], w=[vhat_b[s]])
        for g in range(8):
            psv, psv_b = pp.next()
            for s in range(4):
                S.pe(lambda e, g=g, s=s, psv=psv: e.matmul(psv[:, s * 128:(s + 1) * 128], vhat[:, s, g * 128:(g + 1) * 128],
                                                          WT[:, g, :], start=True, stop=True),
                     r=[vhat_b[s], WT_b], w=[psv_b])
            psu, psu_b = pp.next()
            for k in range(8):
                S.pe(lambda e, g=g, k=k, psu=psu: e.matmul(psu[:], w1[:, k, g * 128:(g + 1) * 128], xnT[:, k, :],
                                                          start=(k == 0), stop=(k == 7)), r=xnT_b + [w1_b], w=[psu_b])
            psg, psg_b = pp.next()
            for k in range(8):
                S.pe(lambda e, g=g, k=k, psg=psg: e.matmul(psg[:], w1[:, k, 2048 + g * 128:2048 + (g + 1) * 128], xnT[:, k, :],
                                                          start=(k == 0), stop=(k == 7)), r=xnT_b + [w1_b], w=[psg_b])
            a1, a1_b = t1.next()
            a2, a2_b = t2.next()
            a3, a3_b = t3.next()
            a4, a4_b = t4.next()
            S.act(lambda e, psu=psu, a1=a1: e.activation(out=a1[:], in_=psu[:], func=AF.Gelu_apprx_tanh), r=[psu_b], w=[a1_b])
            S.act(lambda e, psg=psg, a2=a2: e.activation(out=a2[:], in_=psg[:], func=AF.Silu), r=[psg_b], w=[a2_b])
            S.dve(lambda e, g=g, psv=psv, a3=a3: e.scalar_tensor_tensor(
                out=a3[:].rearrange("p (s i) -> p s i", s=4), in0=psv[:].rearrange("p (s i) -> p s i", s=4),
                scalar=lng[:, g:g + 1], in1=Bsv[:, g, :].unsqueeze(1).to_broadcast([128, 4, 128]),
                op0=ALU.mult, op1=ALU.add), r=[psv_b, lng_b, Bsv_b], w=[a3_b])
            S.pool(lambda e, a1=a1, a2=a2, a4=a4: e.tensor_tensor(out=a4[:], in0=a1[:], in1=a2[:], op=ALU.mult),
                   r=[a1_b, a2_b], w=[a4_b])
            S.dve(lambda e, g=g, a3=a3, a4=a4: e.tensor_tensor(out=yT[:, g, :], in0=a3[:], in1=a4[:], op=ALU.mult),
                  r=[a3_b, a4_b], w=[yT_b[g]])
        S.pool(lambda e: e.memset(ss2[:], 0.0), w=[ss2_b])
        for s in range(4):
            ht, ht_b = hin[:, s, :], hin_b[s]
            for hh in range(2):
                ps, ps_b = pp.next()
                for k in range(8):
                    S.pe(lambda e, s=s, hh=hh, k=k, ps=ps: e.matmul(ps[:], yT[:, k, s * 128:(s + 1) * 128],
                                                                  w2[:, k, hh * 512:(hh + 1) * 512],
                                                                  start=(k == 0), stop=(k == 7)),
                         r=[yT_b[k], w2_b], w=[ps_b])
                S.dve(lambda e, s=s, hh=hh, ps=ps, ht=ht: e.tensor_tensor(out=ht[:, hh * 512:(hh + 1) * 512], in0=ps[:],
                                                                         in1=hin[:, s, hh * 512:(hh + 1) * 512], op=ALU.add),
                      r=[ps_b, hin_b[s]], w=[ht_b])
            S.act(lambda e, s=s, ht=ht: e.activation(out=sq[:], in_=ht[:], func=AF.Square, accum_out=ss2[:, s:s + 1]),
                  r=[ht_b], w=[sq_b, ss2_b])
        rstd_from_sumsq(S, ss2[:], ss2_b, rs2[:], rs2_b, lt[:], lt_b, 1.0 / D, 1e-6)
        for s in range(4):
            o, o_b = ot.next()
            S.dve(lambda e, s=s, o=o: e.scalar_tensor_tensor(out=o[:], in0=hin[:, s, :], scalar=rs2[:, s:s + 1], in1=fgb[:],
                                                           op0=ALU.mult, op1=ALU.mult),
                  r=[hin_b[s], rs2_b, fgb_b], w=[o_b])
            out_ops.append(S.dma("sp", lambda e, s=s, o=o, r0=r0: e.dma_start(
                out=out[r0 + s * 128:r0 + (s + 1) * 128, :], in_=o[:]), r=[o_b]))
    return out_ops


class NormT:
    def __init__(self, S, nc, es, prefix, gb, gb_b, ident, ident_b, pst, pst_b):
        sb = lambda name, shape, dt=F32: es.enter_context(nc.sbuf_tensor(prefix + name, shape, dt))
        self.S = S
        self.gb, self.gb_b, self.ident, self.ident_b, self.pst, self.pst_b = gb, gb_b, ident, ident_b, pst, pst_b
        self.xn = Rot(S, nc, es, 2, [128, 1024], BF16, prefix + "xn")
        self.sq = sb("sq", [128, 1024], BF16); self.sq_b = S.buf("sq")
        self.ss = sb("ss", [128, 4]); self.ss_b = S.buf("ss")
        self.lt = sb("lt", [128, 4]); self.lt_b = S.buf("lt")
        self.rs = sb("rs", [128, 4]); self.rs_b = S.buf("rs")

    def run(self, xin, xin_b, xnT, xnT_b):
        S = self.S
        ss, sq, rs, lt = self.ss, self.sq, self.rs, self.lt
        S.pool(lambda e: e.memset(ss[:], 0.0), w=[self.ss_b])
        for s in range(4):
            S.act(lambda e, s=s: e.activation(out=sq[:], in_=xin[:, s, :], func=AF.Square, accum_out=ss[:, s:s + 1]),
                  r=[xin_b[s]], w=[self.sq_b, self.ss_b])
        rstd_from_sumsq(S, ss[:], self.ss_b, rs[:], self.rs_b, lt[:], self.lt_b, 1.0 / D, 1e-6)
        for s in range(4):
            xt, xt_b = self.xn.next()
            S.dve(lambda e, s=s, xt=xt: e.scalar_tensor_tensor(out=xt[:], in0=xin[:, s, :], scalar=rs[:, s:s + 1],
                                                             in1=self.gb[:], op0=ALU.mult, op1=ALU.mult),
                  r=[xin_b[s], self.rs_b, self.gb_b], w=[xt_b])
            for k in range(8):
                S.pe(lambda e, k=k, xt=xt: e.transpose(self.pst[:, k, :], xt[:, k * 128:(k + 1) * 128], self.ident[:]),
                     r=[xt_b, self.ident_b], w=[self.pst_b])
            S.act(lambda e, s=s: e.copy(out=xnT[:, :, s * 128:(s + 1) * 128], in_=self.pst[:]),
                  r=[self.pst_b], w=[xnT_b[s]])


NEG = -30000.0


def phase1a(S, nc, es0, dr, nb=NB, ntiles_per_seq=4):
    es = contextlib.ExitStack()
    es0.enter_context(es)
    sb = lambda name, shape, dt=F32: es.enter_context(nc.sbuf_tensor("a_" + name, shape, dt))
    x = dr["x"]
    yb = dr["yb"]

    wA = sb("wA", [128, 8, 2048], BF16); wA_b = S.buf("wA")
    ident = sb("ident", [128, 128], BF16); ident_b = S.buf("ident")
    g0b = sb("g0b", [128, 1024]); g0b_b = S.buf()
    bias = sb("bias", [128, 8, 640]); bias_b = S.buf("bias")
    ones = sb("ones", [128, 64], BF16); ones_b = S.buf("ones")
    S.dma("pool", lambda e: e.dma_start(out=ident[:], in_=dr["ident"]), w=[ident_b])
    S.dma("sp", lambda e: e.dma_start(out=g0b[:], in_=dr["norm_g"][0:1, :].to_broadcast([128, D])), w=[g0b_b])
    S.dma("sp", [lambda e, h=h: e.dma_start(out=bias[:, h, :], in_=dr["att_biasT"][:, h, :]) for h in range(8)], w=[bias_b])
    load_weight_bf16(S, wA, wA_b, dr["w_in_e"], 8, 2048, col0=2176, colstep=1024)
    S.pool(lambda e: e.memset(bias[0:64, :, 576:640], NEG), w=[bias_b])
    S.pool(lambda e: e.memset(bias[64:128, :, 0:64], NEG), w=[bias_b])
    S.pool(lambda e: e.memset(ones[:], 1.0), w=[ones_b])

    pp = PsumPool(S, nc, es, 3, "aps")
    pst = es.enter_context(nc.psum_tensor("a_pst", [128, 8, 128], BF16)); pst_b = S.buf("pst")
    psc = [es.enter_context(nc.psum_tensor(f"a_psc{i}", [128, 1024], F32)) for i in range(2)]
    psc_b = [S.buf(f"psc{i}") for i in range(2)]

    xin = [sb(f"xin{i}", [128, 4, 1024]) for i in range(2)]
    xin_b = [[S.buf(f"xin{i}_{s}") for s in range(4)] for i in range(2)]
    xnT = sb("xnT", [128, 8, 512], BF16); xnT_b = [S.buf(f"xnT{s}") for s in range(4)]
    nt = NormT(S, nc, es, "a_", g0b, g0b_b, ident, ident_b, pst, pst_b)
    qT = sb("qT", [128, 4, 512], BF16); qT_b = [S.buf(f"qT{i}") for i in range(4)]
    kT = sb("kT", [128, 4, SEQ], BF16); kT_b = [[S.buf(f"kT{hp}_{t}") for t in range(4)] for hp in range(4)]
    Vs = sb("Vs", [128, 16, 512], BF16); Vs_b = [S.buf(f"V{t}") for t in range(16)]
    sgb = sb("sgb", [128, 4, 512]); sgb_b = [S.buf(f"sgb{i}") for i in range(4)]
    sc = Rot(S, nc, es, 2, [128, 640], F32, "a_sc")
    nmx = Rot(S, nc, es, 4, [128, 1], F32, "a_nmx")
    Pm = Rot(S, nc, es, 2, [128, 640], BF16, "a_P")
    PT = Rot(S, nc, es, 2, [128, 5, 128], BF16, "a_PT")
    rinv = Rot(S, nc, es, 2, [128, 128], F32, "a_rinv")
    rg = Rot(S, nc, es, 2, [128, 128], F32, "a_rg")
    ybT = Rot(S, nc, es, 2, [128, 4, 512], BF16, "a_ybT")

    out_ops = []
    tix = 0
    for b in range(nb):
        for tt in range(ntiles_per_seq):
            r0 = b * SEQ + tt * TT
            xi, xi_b = xin[tix % 2], xin_b[tix % 2]
            tix += 1
            for s in range(4):
                S.dma("sp", lambda e, s=s, r0=r0, xi=xi: e.dma_start(out=xi[:, s, :], in_=x[r0 + s * 128:r0 + (s + 1) * 128, :]),
                      w=[xi_b[s]])
            nt.run(xi, xi_b, xnT, xnT_b)
            for hp in range(4):
                ps, ps_b = pp.next()
                for k in range(8):
                    S.pe(lambda e, hp=hp, k=k, ps=ps: e.matmul(ps[:], wA[:, k, hp * 128:(hp + 1) * 128], xnT[:, k, :],
                                                              start=(k == 0), stop=(k == 7)), r=xnT_b + [wA_b], w=[ps_b])
                S.act(lambda e, hp=hp, ps=ps: e.activation(out=qT[:, hp, :], in_=ps[:], func=AF.Copy, scale=0.125),
                      r=[ps_b], w=[qT_b[hp]])
            for hp in range(4):
                ps, ps_b = pp.next()
                for k in range(8):
                    S.pe(lambda e, hp=hp, k=k, ps=ps: e.matmul(ps[:], wA[:, k, 512 + hp * 128:512 + (hp + 1) * 128], xnT[:, k, :],
                                                              start=(k == 0), stop=(k == 7)), r=xnT_b + [wA_b], w=[ps_b])
                S.dve(lambda e, hp=hp, ps=ps, tt=tt: e.tensor_copy(out=kT[:, hp, tt * TT:(tt + 1) * TT], in_=ps[:]),
                      r=[ps_b], w=[kT_b[hp][tt]])
            for s in range(4):
                ps, ps_b = pp.next()
                for k in range(8):
                    S.pe(lambda e, s=s, k=k, ps=ps: e.matmul(ps[:], xnT[:, k, s * 128:(s + 1) * 128], wA[:, k, 1024:1536],
                                                            start=(k == 0), stop=(k == 7)), r=[xnT_b[s], wA_b], w=[ps_b])
                S.dve(lambda e, s=s, ps=ps, tt=tt: e.tensor_copy(out=Vs[:, tt * 4 + s, :], in_=ps[:]),
                      r=[ps_b], w=[Vs_b[tt * 4 + s]])
            for hp in range(4):
                ps, ps_b = pp.next()
                for k in range(8):
                    S.pe(lambda e, hp=hp, k=k, ps=ps: e.matmul(ps[:], wA[:, k, 1536 + hp * 128:1536 + (hp + 1) * 128], xnT[:, k, :],
                                                              start=(k == 0), stop=(k == 7)), r=xnT_b + [wA_b], w=[ps_b])
                S.act(lambda e, hp=hp, ps=ps: e.activation(out=sgb[:, hp, :], in_=ps[:], func=AF.Silu), r=[ps_b], w=[sgb_b[hp]])
            yt, yt_b = ybT.next()
            for jl in range(4):
                j = tt * 4 + jl
                k0 = max(0, 128 * j - 512)
                k1 = 128 * j + 128
                nk = k1 - k0
                kj0 = k0 - (128 * j - 512)
                nkt = nk // 128
                kt0 = k0 // 128
                krd = [kT_b_hp for kT_b_hp in range(0)]
                for hp in range(4):
                    po, po_b = pp.next()
                    for hh in range(2):
                        h = hp * 2 + hh
                        pb = hh * 64
                        pc, pc_b = psc[(jl * 8 + h) % 2], psc_b[(jl * 8 + h) % 2]
                        kbufs = [kT_b[hp][t] for t in range(k0 // TT, (k1 - 1) // TT + 1)]
                        n1 = min(nk, 512)
                        S.pe(lambda e, hp=hp, pb=pb, jl=jl, pc=pc, k0=k0, n1=n1: e.matmul(
                            pc[:, 0:n1], qT[pb:pb + 64, hp, jl * 128:(jl + 1) * 128], kT[pb:pb + 64, hp, k0:k0 + n1],
                            start=True, stop=True), r=[qT_b[hp]] + kbufs, w=[pc_b])
                        if nk > 512:
                            S.pe(lambda e, hp=hp, pb=pb, jl=jl, pc=pc, k0=k0, nk=nk: e.matmul(
                                pc[:, 512:nk], qT[pb:pb + 64, hp, jl * 128:(jl + 1) * 128], kT[pb:pb + 64, hp, k0 + 512:k0 + nk],
                                start=True, stop=True), r=[qT_b[hp]] + kbufs, w=[pc_b])
                        st, st_b = sc.next()
                        S.dve(lambda e, h=h, pc=pc, st=st, nk=nk, kj0=kj0: e.scalar_tensor_tensor(
                            out=st[:, 0:nk], in0=pc[:, 0:nk], scalar=-1.0, in1=bias[:, h, kj0:kj0 + nk],
                            op0=ALU.mult, op1=ALU.subtract), r=[pc_b, bias_b], w=[st_b])
                        mt, mt_b = nmx.next()
                        S.dve(lambda e, st=st, mt=mt, nk=nk: e.tensor_reduce(out=mt[:], in_=st[:, 0:nk], axis=AX.X, op=ALU.min),
                               r=[st_b], w=[mt_b])
                        pm, pm_b = Pm.next()
                        S.act(lambda e, st=st, mt=mt, pm=pm, nk=nk: e.activation(out=pm[:, 0:nk], in_=st[:, 0:nk], func=AF.Exp,
                                                                               bias=mt[:], scale=-1.0),
                              r=[st_b, mt_b], w=[pm_b])
                        for kt in range(nkt):
                            S.pe(lambda e, kt=kt, pm=pm: e.transpose(pst[:, kt, :], pm[:, kt * 128:(kt + 1) * 128], ident[:]),
                                 r=[pm_b, ident_b], w=[pst_b])
                        pt, pt_b = PT.next()
                        S.act(lambda e, pt=pt, nkt=nkt: e.copy(out=pt[:, 0:nkt, :], in_=pst[:, 0:nkt, :]), r=[pst_b], w=[pt_b])
                        for kt in range(nkt):
                            S.pe(lambda e, kt=kt, pt=pt, po=po, pb=pb, h=h, kt0=kt0, nkt=nkt: e.matmul(
                                po[pb:pb + 64, 0:128], Vs[:, kt0 + kt, h * 64:(h + 1) * 64], pt[:, kt, :],
                                start=(kt == 0), stop=(kt == nkt - 1)), r=[Vs_b[kt0 + kt], pt_b], w=[po_b])
                        for kt in range(nkt):
                            S.pe(lambda e, kt=kt, pt=pt, po=po, pb=pb, nkt=nkt: e.matmul(
                                po[pb:pb + 64, 128:256], ones[:], pt[:, kt, :],
                                start=(kt == 0), stop=(kt == nkt - 1)), r=[ones_b, pt_b], w=[po_b])
                    ri, ri_b = rinv.next()
                    S.dve(lambda e, po=po, ri=ri: e.reciprocal(out=ri[:], in_=po[:, 128:256]), r=[po_b], w=[ri_b])
                    rgt, rgt_b = rg.next()
                    S.pool(lambda e, ri=ri, rgt=rgt, hp=hp, jl=jl: e.tensor_tensor(out=rgt[:], in0=ri[:],
                                                                                in1=sgb[:, hp, jl * 128:(jl + 1) * 128], op=ALU.mult),
                           r=[ri_b, sgb_b[hp]], w=[rgt_b])
                    S.dve(lambda e, po=po, rgt=rgt, yt=yt, hp=hp, jl=jl: e.tensor_tensor(
                        out=yt[:, hp, jl * 128:(jl + 1) * 128], in0=po[:, 0:128], in1=rgt[:], op=ALU.mult),
                        r=[po_b, rgt_b], w=[yt_b])
            out_ops.append(S.dma("sp", lambda e, yt=yt, b=b, tt=tt: e.dma_start(out=yb[b * 4 + tt], in_=yt[:]), r=[yt_b]))
    return out_ops


C0 = float(np.exp(-0.5))
PC_MU, PC_W0, PC_A0, PC_KK, PC_KA, PC_RK, PC_LG, PC_LB, PC_N = 0, 13, 17, 21, 25, 29, 33, 37, 41


def phase1b(S, nc, es0, dr, nb=NB, ntiles_per_seq=4, stage=99, sub=99):
    es = contextlib.ExitStack()
    es0.enter_context(es)
    sb = lambda name, shape, dt=F32: es.enter_context(nc.sbuf_tensor("b_" + name, shape, dt))
    x = dr["x"]
    yb = dr["yb"]
    h1 = dr["h1"]

    wB = sb("wB", [128, 8, 2176], BF16); wB_b = S.buf("wB")
    wO = sb("wO", [128, 8, 1024], BF16); wO_b = S.buf("wO")
    ident = sb("ident", [128, 128], BF16); ident_b = S.buf("ident")
    g0b = sb("g0b", [128, 1024]); g0b_b = S.buf()
    pcol = sb("pcol", [128, PC_N]); pcol_b = S.buf("pcol")
    omm = sb("omm", [128, 13]); omm_b = S.buf("omm")
    omka = sb("omka", [128, 4]); omka_b = S.buf("omka")
    lora = sb("lora", [128, 512]); lora_b = S.buf("lora")
    cm = sb("cm", [128, 512]); cm_b = S.buf("cm")
    id2 = sb("id2", [128, 64], BF16); id2_b = S.buf("id2")
    mask01 = sb("mask01", [128, 512]); mask01_b = S.buf("mask01")
    S.dma("pool", lambda e: e.dma_start(out=ident[:], in_=dr["ident"]), w=[ident_b])
    S.dma("sp", lambda e: e.dma_start(out=g0b[:], in_=dr["norm_g"][0:1, :].to_broadcast([128, D])), w=[g0b_b])
    S.dma("sp", lambda e: e.dma_start(out=pcol[:], in_=dr["pcol"]), w=[pcol_b])
    S.dma("sp", lambda e: e.dma_start(out=lora[:], in_=dr["lora"]), w=[lora_b])
    S.dma("sp", lambda e: e.dma_start(out=cm[:], in_=dr["cmask"]), w=[cm_b])
    S.dma("pool", lambda e: e.dma_start(out=id2[:], in_=dr["cmask"][:, 320:384]), w=[id2_b])
    load_weight_bf16(S, wB, wB_b, dr["w_in_e"], 8, 2176, col0=0, colstep=1088)
    load_weight_bf16(S, wO, wO_b, dr["w_out_e"], 8, 1024)
    S.pool(lambda e: e.tensor_scalar(out=omm[:], in0=pcol[:, PC_MU:PC_MU + 13], scalar1=-1.0, scalar2=1.0,
                                     op0=ALU.mult, op1=ALU.add), r=[pcol_b], w=[omm_b])
    S.pool(lambda e: e.tensor_scalar(out=omka[:], in0=pcol[:, PC_KA:PC_KA + 4], scalar1=-1.0, scalar2=1.0,
                                     op0=ALU.mult, op1=ALU.add), r=[pcol_b], w=[omka_b])
    S.pool(lambda e: e.memset(mask01[:], 1.0), w=[mask01_b])
    S.pool(lambda e: e.memset(mask01[:].rearrange("p (n j) -> p n j", j=64)[:, :, 0:1], 0.0), w=[mask01_b])
    Mlt = cm[:, 0:256]
    Mq = cm[:, 256:320]
    BD = cm[:, 384:512]

    pp = PsumPool(S, nc, es, 2, "bps")
    Zc = [es.enter_context(nc.psum_tensor(f"b_Z{i}", [128, 2, 512], F32)) for i in range(2)]
    Zc_b = [[S.buf(f"Z{i}_{k}") for k in range(2)] for i in range(2)]
    pXUt = es.enter_context(nc.psum_tensor("b_pXU", [128, 512], F32))
    pXU = pXUt[:].rearrange("p (c h v) -> p c h v", c=4, h=2)
    XU_b = [S.buf("pXU0"), S.buf("pXU1")]
    pYSt = es.enter_context(nc.psum_tensor("b_pYS", [128, 512], F32))
    pYS = pYSt[:].rearrange("p (a c v) -> p a c v", a=2, c=4)
    YS_b = S.buf("pYS")

    xin = sb("xin", [128, 4, 1024]); xin_b = [S.buf(f"xin{s}") for s in range(4)]
    xnT = sb("xnT", [128, 8, 512], BF16); xnT_b = [S.buf(f"xnT{s}") for s in range(4)]
    pst0 = es.enter_context(nc.psum_tensor("b_pst", [128, 8, 128], BF16)) if False else None
    ycat = sb("ycat", [128, 8, 512], BF16); ycat_b = [S.buf(f"ycat{c}") for c in range(8)]
    carry = sb("carry", [128, 13]); carry_b = [S.buf(f"carry{i}") for i in range(13)]
    pm = Rot(S, nc, es, 1, [128, 513], F32, "b_pm")
    wdad = sb("wdad", [128, 512]); wdad_b = S.buf("wdad")
    th = sb("th", [64, 512]); th_b = S.buf("th")
    tp = Rot(S, nc, es, 12, [128, 512], F32, "b_tp")
    tb = Rot(S, nc, es, 4, [128, 512], BF16, "b_tb")
    AR = [sb(f"AR{c}", [128, 8, 2, 64], BF16) for c in range(4)]; AR_b = [S.buf(f"AR{c}") for c in range(4)]
    KB = [sb(f"KB{c}", [128, 8, 2, 64], BF16) for c in range(4)]; KB_b = [S.buf(f"KB{c}") for c in range(4)]
    Khat = sb("Khat", [128, 4, 512], BF16); Khat_b = [S.buf(f"Khat{c}") for c in range(4)]
    Bhat = sb("Bhat", [128, 4, 512], BF16); Bhat_b = [S.buf(f"Bhat{c}") for c in range(4)]
    Vtok = sb("Vtok", [128, 4, 512], BF16); Vtok_b = [S.buf(f"Vtok{c}") for c in range(4)]
    WC = sb("WC", [128, 4, 8]); WC_b = [S.buf(f"WC{c}") for c in range(4)]
    sga = sb("sga", [128, 4, 512]); sga_b = [S.buf(f"sga{c}") for c in range(4)]
    bon = sb("bon", [128, 4, 512]); bon_b = [S.buf(f"bon{c}") for c in range(4)]
    ysb = sb("ysb", [128, 4, 512]); ysb_b = [S.buf(f"ysb{c}") for c in range(4)]
    ST = sb("ST", [128, 4, 64]); ST_b = S.buf("ST")
    STt = sb("STt", [128, 4, 64]); STt_b = S.buf("STt")
    S0b = sb("S0b", [128, 4, 64], BF16); S0b_b = S.buf("S0b")
    LTb = Rot(S, nc, es, 5, [128, 2, 256], BF16, "b_LTb")
    Q0 = Rot(S, nc, es, 4, [128, 2, 64], BF16, "b_Q0")
    ZQ = Rot(S, nc, es, 5, [128, 2, 192], BF16, "b_ZQ")
    TTr = Rot(S, nc, es, 6, [128, 2, 64], BF16, "b_TT")
    Xsb = Rot(S, nc, es, 2, [128, 4, 2, 64], BF16, "b_Xsb")
    Usb = Rot(S, nc, es, 2, [128, 4, 2, 64], BF16, "b_Usb")
    pstd = es.enter_context(nc.psum_tensor("b_pstx", [128, 8, 128], BF16)) if False else None

    def pst_view(ps):
        return ps[:].bitcast(BF16).rearrange("p (k j) -> p k j", j=128)

    class _NT(NormT):
        pass

    nt_sq = sb("nsq", [128, 1024], BF16); nt_sq_b = S.buf("nsq")
    nt_ss = sb("nss", [128, 4]); nt_ss_b = S.buf("nss")
    nt_lt = sb("nlt", [128, 4]); nt_lt_b = S.buf("nlt")
    nt_rs = sb("nrs", [128, 4]); nt_rs_b = S.buf("nrs")
    xn = Rot(S, nc, es, 2, [128, 1024], BF16, "b_xn")

    def norm_transpose():
        S.pool(lambda e: e.memset(nt_ss[:], 0.0), w=[nt_ss_b])
        for s in range(4):
            S.act(lambda e, s=s: e.activation(out=nt_sq[:], in_=xin[:, s, :], func=AF.Square, accum_out=nt_ss[:, s:s + 1]),
                  r=[xin_b[s]], w=[nt_sq_b, nt_ss_b])
        rstd_from_sumsq(S, nt_ss[:], nt_ss_b, nt_rs[:], nt_rs_b, nt_lt[:], nt_lt_b, 1.0 / D, 1e-6)
        for s in range(4):
            xt, xt_b = xn.next()
            S.dve(lambda e, s=s, xt=xt: e.scalar_tensor_tensor(out=xt[:], in0=xin[:, s, :], scalar=nt_rs[:, s:s + 1],
                                                             in1=g0b[:], op0=ALU.mult, op1=ALU.mult),
                  r=[xin_b[s], nt_rs_b, g0b_b], w=[xt_b])
            ps, ps_b = pp.next()
            pv = pst_view(ps)
            for k in range(8):
                S.pe(lambda e, k=k, xt=xt, pv=pv: e.transpose(pv[:, k, :], xt[:, k * 128:(k + 1) * 128], ident[:]),
                     r=[xt_b, ident_b], w=[ps_b])
            S.act(lambda e, s=s, pv=pv: e.copy(out=xnT[:, :, s * 128:(s + 1) * 128], in_=pv), r=[ps_b], w=[xnT_b[s]])

    def inproj_fm(ci):
        ps, ps_b = pp.next()
        for k in range(8):
            S.pe(lambda e, k=k, ps=ps: e.matmul(ps[:], wB[:, k, ci * 128:(ci + 1) * 128], xnT[:, k, :],
                                               start=(k == 0), stop=(k == 7)), r=xnT_b + [wB_b], w=[ps_b])
        return ps, ps_b

    def shifted(ci, dst, dst_b, first):
        ps, ps_b = inproj_fm(ci)
        pmt, pmt_b = pm.next()
        if first:
            S.pool(lambda e, pmt=pmt: e.memset(pmt[:, 0:1], 0.0), w=[pmt_b])
        else:
            S.pool(lambda e, pmt=pmt: e.tensor_copy(out=pmt[:, 0:1], in_=carry[:, ci:ci + 1]), r=[carry_b[ci]], w=[pmt_b])
        S.act(lambda e, pmt=pmt, ps=ps: e.activation(out=pmt[:, 1:513], in_=ps[:], func=AF.Copy,
                                                    scale=pcol[:, PC_MU + ci:PC_MU + ci + 1]),
              r=[ps_b, pcol_b], w=[pmt_b])
        S.dve(lambda e, pmt=pmt, ps=ps: e.scalar_tensor_tensor(out=dst, in0=ps[:], scalar=omm[:, ci:ci + 1], in1=pmt[:, 0:512],
                                                              op0=ALU.mult, op1=ALU.add),
              r=[ps_b, omm_b, pmt_b], w=[dst_b])
        S.pool(lambda e, pmt=pmt: e.tensor_copy(out=carry[:, ci:ci + 1], in_=pmt[:, 512:513]), r=[pmt_b], w=[carry_b[ci]])

    def v3(ap):
        return ap.rearrange("p (n j) -> p n j", j=64)

    out_ops = []
    for b in range(nb):
        for tt in range(ntiles_per_seq):
            r0 = b * SEQ + tt * TT
            first = (tt == 0)
            for s in range(4):
                S.dma("sp", lambda e, s=s, r0=r0: e.dma_start(out=xin[:, s, :], in_=x[r0 + s * 128:r0 + (s + 1) * 128, :]),
                      w=[xin_b[s]])
            S.dma("sp", lambda e, b=b, tt=tt: e.dma_start(out=ycat[:, 4:8, :], in_=yb[b * 4 + tt]), w=ycat_b[4:8])
            norm_transpose()
            if first:
                S.pool(lambda e: e.memset(ST[:], 0.0), w=[ST_b])
                S.pool(lambda e: e.memset(S0b[:], 0.0), w=[S0b_b])
            shifted(12, wdad[:], wdad_b, first)
            S.act(lambda e: e.activation(out=th[:], in_=wdad[0:64, :], func=AF.Tanh), r=[wdad_b], w=[th_b])
            if stage == 1:
                return out_ops
            for c in range(4):
                tp.i = 0
                ps, ps_b = pp.next()
                S.pe(lambda e, c=c, ps=ps: e.matmul(ps[:], lora[0:64, c * 128:(c + 1) * 128], th[0:64, :], start=True, stop=True),
                     r=[lora_b, th_b], w=[ps_b])
                sig, sig_b = tp.next()
                S.act(lambda e, c=c, ps=ps, sig=sig: e.activation(out=sig[:], in_=ps[:], func=AF.Sigmoid,
                                                                 bias=pcol[:, PC_W0 + c:PC_W0 + c + 1], scale=1.0),
                      r=[ps_b, pcol_b], w=[sig_b])
                ps, ps_b = pp.next()
                S.pe(lambda e, c=c, ps=ps: e.matmul(ps[:], lora[64:128, c * 128:(c + 1) * 128], wdad[64:128, :], start=True, stop=True),
                     r=[lora_b, wdad_b], w=[ps_b])
                av, av_b = tp.next()
                S.act(lambda e, c=c, ps=ps, av=av: e.activation(out=av[:], in_=ps[:], func=AF.Sigmoid,
                                                               bias=pcol[:, PC_A0 + c:PC_A0 + c + 1], scale=1.0),
                      r=[ps_b, pcol_b], w=[av_b])
                rr, rr_b = tp.next()
                kx, kx_b = tp.next()
                vv, vv_b = tp.next()
                shifted(c, rr[:], rr_b, first)
                shifted(4 + c, kx[:], kx_b, first)
                shifted(8 + c, vv[:], vv_b, first)
                ps, ps_b = inproj_fm(13 + c)
                S.act(lambda e, c=c, ps=ps: e.activation(out=sga[:, c, :], in_=ps[:], func=AF.Silu), r=[ps_b], w=[sga_b[c]])
                cs, cs_b = tp.next()
                S.dve(lambda e, cs=cs, sig=sig: e.tensor_tensor_scan(out=cs[:], data0=mask01[:], data1=sig[:], initial=0.0,
                                                                   op0=ALU.mult, op1=ALU.add),
                      r=[mask01_b, sig_b], w=[cs_b])
                E1, E1_b = tp.next()
                E2, E2_b = tp.next()
                E3, E3_b = tp.next()
                E4, E4_b = tp.next()
                t0, t0_b = tp.next()
                t1, t1_b = tp.next()
                S.act(lambda e, cs=cs, E1=E1: e.activation(out=E1[:], in_=cs[:], func=AF.Exp, scale=-C0), r=[cs_b], w=[E1_b])
                S.act(lambda e, cs=cs, E2=E2: e.activation(out=E2[:], in_=cs[:], func=AF.Exp, scale=C0), r=[cs_b], w=[E2_b])
                S.pool(lambda e, cs=cs, sig=sig, t0=t0: e.tensor_tensor(out=t0[:], in0=cs[:], in1=sig[:], op=ALU.subtract),
                       r=[cs_b, sig_b], w=[t0_b])
                S.act(lambda e, t0=t0, E3=E3: e.activation(out=E3[:], in_=t0[:], func=AF.Exp, scale=-C0), r=[t0_b], w=[E3_b])
                S.pool(lambda e, cs=cs, t1=t1: e.tensor_tensor(out=v3(t1[:]), in0=v3(cs[:]),
                                                              in1=v3(cs[:])[:, :, 63:64].to_broadcast([128, 8, 64]),
                                                              op=ALU.subtract), r=[cs_b], w=[t1_b])
                S.act(lambda e, t1=t1, E4=E4: e.activation(out=E4[:], in_=t1[:], func=AF.Exp, scale=C0), r=[t1_b], w=[E4_b])
                S.pool(lambda e, c=c, E1=E1: e.tensor_copy(out=WC[:, c, :], in_=v3(E1[:])[:, :, 63]), r=[E1_b], w=[WC_b[c]])
                kk, kk_b = tp.next()
                S.pool(lambda e, c=c, kk=kk, kx=kx: e.tensor_scalar(out=kk[:], in0=kx[:], scalar1=pcol[:, PC_KK + c:PC_KK + c + 1],
                                                                   scalar2=None, op0=ALU.mult), r=[kx_b, pcol_b], w=[kk_b])
                S.pool(lambda e, kk=kk, t0=t0: e.tensor_tensor(out=t0[:], in0=kk[:], in1=kk[:], op=ALU.mult), r=[kk_b], w=[t0_b])
                ps, ps_b = pp.next()
                S.pe(lambda e, ps=ps, t0=t0: e.matmul(ps[:], BD, t0[:], start=True, stop=True), r=[cm_b, t0_b], w=[ps_b])
                S.dve(lambda e, ps=ps, t1=t1: e.tensor_scalar(out=t1[:], in0=ps[:], scalar1=1e-24, scalar2=None, op0=ALU.max),
                      r=[ps_b], w=[t1_b])
                S.act(lambda e, t1=t1: e.activation(out=t1[:], in_=t1[:], func=AF.Ln), r=[t1_b], w=[t1_b])
                S.act(lambda e, t1=t1: e.activation(out=t1[:], in_=t1[:], func=AF.Exp, scale=-0.5), r=[t1_b], w=[t1_b])
                S.dve(lambda e, kk=kk, t1=t1: e.tensor_tensor(out=kk[:], in0=kk[:], in1=t1[:], op=ALU.mult),
                      r=[kk_b, t1_b], w=[kk_b])
                S.dve(lambda e, c=c, kk=kk, E3=E3: e.scalar_tensor_tensor(out=AR[c][:, :, 0, :], in0=v3(kk[:]), scalar=-1.0,
                                                                        in1=v3(E3[:]), op0=ALU.mult, op1=ALU.mult),
                      r=[kk_b, E3_b], w=[AR_b[c]])
                S.pool(lambda e, kk=kk, av=av, t0=t0: e.tensor_tensor(out=t0[:], in0=kk[:], in1=av[:], op=ALU.mult),
                       r=[kk_b, av_b], w=[t0_b])
                S.dve(lambda e, c=c, t0=t0, E2=E2: e.tensor_tensor(out=KB[c][:, :, 1, :], in0=v3(t0[:]), in1=v3(E2[:]), op=ALU.mult),
                      r=[t0_b, E2_b], w=[KB_b[c]])
                bh, bh_b = tb.next()
                S.pool(lambda e, t0=t0, E4=E4, bh=bh: e.tensor_tensor(out=bh[:], in0=t0[:], in1=E4[:], op=ALU.mult),
                       r=[t0_b, E4_b], w=[bh_b])
                S.pool(lambda e, c=c, av=av, t1=t1: e.tensor_scalar(out=t1[:], in0=av[:], scalar1=pcol[:, PC_KA + c:PC_KA + c + 1],
                                                                   scalar2=omka[:, c:c + 1], op0=ALU.mult, op1=ALU.add),
                       r=[av_b, pcol_b, omka_b], w=[t1_b])
                S.dve(lambda e, kx=kx, t1=t1: e.tensor_tensor(out=t1[:], in0=kx[:], in1=t1[:], op=ALU.mult),
                      r=[kx_b, t1_b], w=[t1_b])
                S.dve(lambda e, c=c, t1=t1, E2=E2: e.tensor_tensor(out=KB[c][:, :, 0, :], in0=v3(t1[:]), in1=v3(E2[:]), op=ALU.mult),
                      r=[t1_b, E2_b], w=[KB_b[c]])
                kh, kh_b = tb.next()
                S.pool(lambda e, t1=t1, E4=E4, kh=kh: e.tensor_tensor(out=kh[:], in0=t1[:], in1=E4[:], op=ALU.mult),
                       r=[t1_b, E4_b], w=[kh_b])
                S.dve(lambda e, c=c, rr=rr, E1=E1: e.tensor_tensor(out=AR[c][:, :, 1, :], in0=v3(rr[:]), in1=v3(E1[:]), op=ALU.mult),
                      r=[rr_b, E1_b], w=[AR_b[c]])
                S.dve(lambda e, c=c, rr=rr, t1=t1, t0=t0: e.scalar_tensor_tensor(out=t0[:], in0=rr[:],
                                                                               scalar=pcol[:, PC_RK + c:PC_RK + c + 1], in1=t1[:],
                                                                               op0=ALU.mult, op1=ALU.mult),
                      r=[rr_b, t1_b, pcol_b], w=[t0_b])
                ps, ps_b = pp.next()
                S.pe(lambda e, ps=ps, t0=t0: e.matmul(ps[:], BD, t0[:], start=True, stop=True), r=[cm_b, t0_b], w=[ps_b])
                S.dve(lambda e, c=c, ps=ps, vv=vv: e.tensor_tensor(out=bon[:, c, :], in0=ps[:], in1=vv[:], op=ALU.mult),
                      r=[ps_b, vv_b], w=[bon_b[c]])
                vb, vb_b = tb.next()
                S.act(lambda e, vv=vv, vb=vb: e.copy(out=vb[:], in_=vv[:]), r=[vv_b], w=[vb_b])
                for src, src_b, dst, dst_b in ((kh, kh_b, Khat, Khat_b), (bh, bh_b, Bhat, Bhat_b), (vb, vb_b, Vtok, Vtok_b)):
                    ps, ps_b = pp.next()
                    pv = pst_view(ps)
                    for s in range(4):
                        S.pe(lambda e, s=s, src=src, pv=pv: e.transpose(pv[:, s, :], src[:, s * 128:(s + 1) * 128], ident[:]),
                             r=[src_b, ident_b], w=[ps_b])
                    S.act(lambda e, c=c, dst=dst, pv=pv: e.copy(out=dst[:, :, c * 128:(c + 1) * 128], in_=pv[:, 0:4, :]),
                          r=[ps_b], w=[dst_b[c]])

            if stage == 2:
                return out_ops
            def grouped(entries):
                order = sorted(range(len(entries)), key=lambda i: entries[i][1])
                cnt = {}
                for i in order:
                    cnt[entries[i][0]] = cnt.get(entries[i][0], 0) + 1
                seen = {}
                zr = set()
                for i in order:
                    reg, row, out_, l_, r_, rb, wb = entries[i]
                    k = seen.get(reg, 0)
                    seen[reg] = k + 1
                    oa = out_.ap
                    zkey = (out_.tensor.name, ((out_.offset % oa[0][0]) * 4) // 2048, (out_.offset // oa[0][0]) // 64)
                    first = zkey not in zr
                    zr.add(zkey)
                    S.mm(out_, l_, r_, start=first, stop=(k == cnt[reg] - 1), r=rb, w=wb, skip=True)

            for m in range(4):
                lts, tts = {}, {}
                for grp in range(2):
                    cs_ = (2 * grp, 2 * grp + 1)
                    q0 = {}
                    for ci_, c in enumerate(cs_):
                        pL = Zc[ci_]
                        pL_b = Zc_b[ci_]
                        for hh in range(2):
                            pb = hh * 64
                            for par in range(2):
                                n = 2 * m + par
                                pbn = par * 64
                                arn = AR[c][pb:pb + 64, n, :, :].rearrange("p a j -> p (a j)")
                                S.mm(pL[pbn:pbn + 64, hh, 0:128], KB[c][pb:pb + 64, n, 0, :], arn, r=[KB_b[c], AR_b[c]], w=[pL_b[hh]])
                                S.mm(pL[pbn:pbn + 64, hh, 128:256], KB[c][pb:pb + 64, n, 1, :], arn, r=[KB_b[c], AR_b[c]], w=[pL_b[hh]])
                                S.mm(pL[pbn:pbn + 64, hh, 256:320], AR[c][pb:pb + 64, n, 0, :], KB[c][pb:pb + 64, n, 1, :],
                                     r=[KB_b[c], AR_b[c]], w=[pL_b[hh]])
                        if stage == 3 and sub == 0:
                            return out_ops
                        lt, lt_b = LTb.next()
                        S.dve(lambda e, lt=lt, pL=pL: e.tensor_tensor(out=lt[:], in0=pL[:, :, 0:256],
                                                              in1=Mlt.unsqueeze(1).to_broadcast([128, 2, 256]),
                                                              op=ALU.mult), r=pL_b + [cm_b], w=[lt_b])
                        qt, qt_b = Q0.next()
                        S.dve(lambda e, qt=qt, pL=pL: e.tensor_tensor(out=qt[:], in0=pL[:, :, 256:320],
                                                                    in1=Mq.unsqueeze(1).to_broadcast([128, 2, 64]), op=ALU.mult),
                              r=pL_b + [cm_b], w=[qt_b])
                        lts[c] = (lt, lt_b)
                        q0[c] = (qt, qt_b)
                    if stage == 3 and sub == 1:
                        return out_ops
                    cur = {}
                    for level in range(6):
                        for ci_, c in enumerate(cs_):
                            lt, lt_b = lts[c]
                            if level == 0:
                                qt, qt_b = q0[c]
                                Pi = lambda pbn, hh, lt=lt: lt[pbn:pbn + 64, hh, 128:192]
                                Qi = lambda pbn, hh, qt=qt: qt[pbn:pbn + 64, hh, :]
                                Ai = None
                                src_b = [lt_b, qt_b]
                            else:
                                z, z_b = cur[c]
                                Pi = lambda pbn, hh, z=z: z[pbn:pbn + 64, hh, 64:128]
                                Qi = lambda pbn, hh, z=z: z[pbn:pbn + 64, hh, 0:64]
                                Ai = lambda pbn, hh, z=z: z[pbn:pbn + 64, hh, 128:192]
                                src_b = [z_b]
                            for par in range(2):
                                pbn = par * 64
                                pzz = Zc[ci_][:, par, 0:384].rearrange("p (h j) -> p h j", h=2)
                                pz_b = Zc_b[ci_][par]
                                for hh in range(2):
                                    if level < 5:
                                        S.mm(pzz[pbn:pbn + 64, hh, 64:128], Qi(pbn, hh), Pi(pbn, hh), r=src_b, w=[pz_b])
                                        S.mm(pzz[pbn:pbn + 64, hh, 0:64], Pi(pbn, hh), Qi(pbn, hh), r=src_b, w=[pz_b])
                                    if level > 0:
                                        S.mm(pzz[pbn:pbn + 64, hh, 128:192], Qi(pbn, hh), Ai(pbn, hh), start=True, stop=False,
                                             r=src_b, w=[pz_b])
                                        S.mm(pzz[pbn:pbn + 64, hh, 128:192], id2[pbn:pbn + 64, :], Ai(pbn, hh), start=False, stop=True,
                                             r=src_b + [id2_b], w=[pz_b])
                            if stage == 3 and sub == 3 + level and ci_ == 1:
                                return out_ops
                            pzs = [Zc[ci_][:, par, 0:384].rearrange("p (h j) -> p h j", h=2) for par in range(2)]
                            pzb = Zc_b[ci_]
                            if level < 5:
                                zn, zn_b = ZQ.next()
                                for par in range(2):
                                    pbn = par * 64
                                    if level == 0:
                                        S.act(lambda e, zn=zn, pbn=pbn, pzz=pzs[par]: e.copy(out=zn[pbn:pbn + 64, :, 0:128],
                                                                                             in_=pzz[pbn:pbn + 64, :, 0:128]),
                                              r=[pzb[par]], w=[zn_b])
                                    elif (level + par) % 2 == 0:
                                        S.act(lambda e, zn=zn, pbn=pbn, pzz=pzs[par]: e.copy(out=zn[pbn:pbn + 64], in_=pzz[pbn:pbn + 64]),
                                              r=[pzb[par]], w=[zn_b])
                                    else:
                                        S.dve(lambda e, zn=zn, pbn=pbn, pzz=pzs[par]: e.tensor_copy(out=zn[pbn:pbn + 64], in_=pzz[pbn:pbn + 64]),
                                              r=[pzb[par]], w=[zn_b])
                                if level == 0:
                                    S.dve(lambda e, zn=zn, lt=lt: e.tensor_tensor(
                                        out=zn[:, :, 128:192], in0=lt[:, :, 128:192], in1=id2[:].unsqueeze(1).to_broadcast([128, 2, 64]),
                                        op=ALU.add), r=[lt_b, id2_b], w=[zn_b])
                                cur[c] = (zn, zn_b)
                            else:
                                t_, t_b = TTr.next()
                                for par in range(2):
                                    pbn = par * 64
                                    eng = S.dve if par == 0 else S.act
                                    if par == 0:
                                        S.dve(lambda e, t_=t_, pbn=pbn, pzz=pzs[par]: e.tensor_copy(out=t_[pbn:pbn + 64],
                                                                                                   in_=pzz[pbn:pbn + 64, :, 128:192]),
                                              r=[pzb[par]], w=[t_b])
                                    else:
                                        S.act(lambda e, t_=t_, pbn=pbn, pzz=pzs[par]: e.copy(out=t_[pbn:pbn + 64],
                                                                                            in_=pzz[pbn:pbn + 64, :, 128:192]),
                                              r=[pzb[par]], w=[t_b])
                                tts[c] = (t_, t_b)
                if stage == 3:
                    return out_ops
                for par in range(2):
                    n = 2 * m + par
                    pbn = par * 64
                    sub_ = n // 2
                    xs, xs_b = Xsb.next()
                    us, us_b = Usb.next()
                    ent = []
                    for c in range(4):
                        lt, lt_b = lts[c]
                        for hh in range(2):
                            pb = hh * 64
                            col = c * 128 + hh * 64
                            ent.append(((c, hh), pb, pXU[pbn:pbn + 64, c, hh, :], AR[c][pb:pb + 64, n, 0, :], S0b[pb:pb + 64, c, :],
                                        [AR_b[c], S0b_b], [XU_b[par]]))
                            ent.append(((c, hh), pbn, pXU[pbn:pbn + 64, c, hh, :], lt[pbn:pbn + 64, hh, 0:64],
                                        Vtok[pbn:pbn + 64, sub_, col:col + 64], [lt_b, Vtok_b[c]], [XU_b[par]]))
                    grouped(ent)
                    if m == 0 and b == 0 and tt == 0: print("SEQ afterXmm", par, S.nseq)
                    S.act(lambda e, pbn=pbn, xs=xs, par=par: e.copy(out=xs[pbn:pbn + 64], in_=pXU[pbn:pbn + 64]), r=[XU_b[par]], w=[xs_b])
                    for c in range(4):
                        t_, t_b = tts[c]
                        for hh in range(2):
                            S.mm(pXU[pbn:pbn + 64, c, hh, :], t_[pbn:pbn + 64, hh, :], xs[pbn:pbn + 64, c, hh, :],
                                 r=[t_b, xs_b], w=[XU_b[par]])
                    S.dve(lambda e, pbn=pbn, us=us, par=par: e.tensor_copy(out=us[pbn:pbn + 64], in_=pXU[pbn:pbn + 64]), r=[XU_b[par]], w=[us_b])
                    if m == 0 and b == 0 and tt == 0: print("SEQ afterUevac", par, S.nseq)
                    ent = []
                    for c in range(4):
                        lt, lt_b = lts[c]
                        for hh in range(2):
                            pb = hh * 64
                            col = c * 128 + hh * 64
                            yreg = pYS[pb:pb + 64, 0, c, :]
                            sreg = pYS[pb:pb + 64, 1, c, :]
                            ent.append((("y", c, hh), pb, yreg, S0b[pb:pb + 64, c, :], AR[c][pb:pb + 64, n, 1, :], [S0b_b, AR_b[c]], [YS_b]))
                            ent.append((("y", c, hh), pbn, yreg, Vtok[pbn:pbn + 64, sub_, col:col + 64], lt[pbn:pbn + 64, hh, 64:128],
                                        [Vtok_b[c], lt_b], [YS_b]))
                            ent.append((("y", c, hh), pbn, yreg, us[pbn:pbn + 64, c, hh, :], lt[pbn:pbn + 64, hh, 192:256],
                                        [us_b, lt_b], [YS_b]))
                            ent.append((("s", c, hh), pbn, sreg, Khat[pbn:pbn + 64, sub_, col:col + 64], Vtok[pbn:pbn + 64, sub_, col:col + 64],
                                        [Khat_b[c], Vtok_b[c]], [YS_b]))
                            ent.append((("s", c, hh), pbn, sreg, Bhat[pbn:pbn + 64, sub_, col:col + 64], us[pbn:pbn + 64, c, hh, :],
                                        [Bhat_b[c], us_b], [YS_b]))
                    grouped(ent)
                    if m == 0 and b == 0 and tt == 0: print("SEQ afterYSmm", par, S.nseq)
                    S.dve(lambda e, n=n: e.tensor_copy(out=ysb[:, :, n * 64:(n + 1) * 64], in_=pYS[:, 0, :, :]), r=[YS_b], w=ysb_b)
                    for c in range(4):
                        S.dve(lambda e, n=n, c=c: e.scalar_tensor_tensor(out=ST[:, c, :], in0=ST[:, c, :], scalar=WC[:, c, n:n + 1],
                                                                       in1=pYS[:, 1, c, :], op0=ALU.mult, op1=ALU.add),
                              r=[ST_b, WC_b[c], YS_b], w=[ST_b])
                    S.act(lambda e: e.copy(out=S0b[:], in_=ST[:]), r=[ST_b], w=[S0b_b])
                    if m == 0 and b == 0 and tt == 0: print("SEQ afterState", par, S.nseq)

            if stage == 4:
                return out_ops
            for c in range(4):
                ysq, ysq_b = tp.next()
                S.pool(lambda e, c=c, ysq=ysq: e.tensor_tensor(out=ysq[:], in0=ysb[:, c, :], in1=ysb[:, c, :], op=ALU.mult),
                       r=[ysb_b[c]], w=[ysq_b])
                p1, p1_b = pp.next()
                S.pe(lambda e, c=c, p1=p1: e.matmul(p1[:], BD, ysb[:, c, :], start=True, stop=True), r=[cm_b, ysb_b[c]], w=[p1_b])
                p2, p2_b = pp.next()
                S.pe(lambda e, ysq=ysq, p2=p2: e.matmul(p2[:], BD, ysq[:], start=True, stop=True), r=[cm_b, ysq_b], w=[p2_b])
                msq, msq_b = tp.next()
                mean, mean_b = tp.next()
                S.dve(lambda e, p1=p1, mean=mean: e.tensor_scalar(out=mean[:], in0=p1[:], scalar1=1.0 / 64, scalar2=None, op0=ALU.mult),
                      r=[p1_b], w=[mean_b])
                S.pool(lambda e, mean=mean, msq=msq: e.tensor_tensor(out=msq[:], in0=mean[:], in1=mean[:], op=ALU.mult),
                       r=[mean_b], w=[msq_b])
                var, var_b = tp.next()
                S.dve(lambda e, p2=p2, msq=msq, var=var: e.scalar_tensor_tensor(out=var[:], in0=p2[:], scalar=1.0 / 64, in1=msq[:],
                                                                              op0=ALU.mult, op1=ALU.subtract),
                      r=[p2_b, msq_b], w=[var_b])
                S.act(lambda e, var=var: e.activation(out=var[:], in_=var[:], func=AF.Ln, bias=64e-5, scale=1.0), r=[var_b], w=[var_b])
                S.act(lambda e, var=var: e.activation(out=var[:], in_=var[:], func=AF.Exp, scale=-0.5), r=[var_b], w=[var_b])
                dy, dy_b = tp.next()
                S.pool(lambda e, c=c, mean=mean, dy=dy: e.tensor_tensor(out=dy[:], in0=ysb[:, c, :], in1=mean[:], op=ALU.subtract),
                       r=[mean_b, ysb_b[c]], w=[dy_b])
                S.pool(lambda e, dy=dy, var=var: e.tensor_tensor(out=dy[:], in0=dy[:], in1=var[:], op=ALU.mult),
                       r=[dy_b, var_b], w=[dy_b])
                S.pool(lambda e, c=c, dy=dy: e.tensor_scalar(out=dy[:], in0=dy[:], scalar1=pcol[:, PC_LG + c:PC_LG + c + 1],
                                                            scalar2=pcol[:, PC_LB + c:PC_LB + c + 1], op0=ALU.mult, op1=ALU.add),
                       r=[dy_b, pcol_b], w=[dy_b])
                S.pool(lambda e, c=c, dy=dy: e.tensor_tensor(out=dy[:], in0=dy[:], in1=bon[:, c, :], op=ALU.add),
                       r=[dy_b, bon_b[c]], w=[dy_b])
                S.dve(lambda e, c=c, dy=dy: e.tensor_tensor(out=ycat[:, c, :], in0=dy[:], in1=sga[:, c, :], op=ALU.mult),
                      r=[dy_b, sga_b[c]], w=[ycat_b[c]])

            for s in range(4):
                h, h_b = xin[:, s, :], xin_b[s]
                for hh in range(2):
                    ps, ps_b = pp.next()
                    for k in range(8):
                        S.pe(lambda e, s=s, hh=hh, k=k, ps=ps: e.matmul(ps[:], ycat[:, k, s * 128:(s + 1) * 128],
                                                                      wO[:, k, hh * 512:(hh + 1) * 512],
                                                                      start=(k == 0), stop=(k == 7)),
                             r=[ycat_b[k], wO_b], w=[ps_b])
                    S.dve(lambda e, s=s, hh=hh, ps=ps, h=h: e.tensor_tensor(out=h[:, hh * 512:(hh + 1) * 512], in0=ps[:],
                                                                          in1=xin[:, s, hh * 512:(hh + 1) * 512], op=ALU.add),
                          r=[ps_b, xin_b[s]], w=[h_b])
                out_ops.append(S.dma("sp", lambda e, s=s, h=h, r0=r0: e.dma_start(
                    out=h1[r0 + s * 128:r0 + (s + 1) * 128, :], in_=h), r=[h_b]))
    return out_ops


def build_program():
    nc = bass.Bass("TRN2", target_bir_lowering=False)

    def din(name, shape, dt=F32):
        return nc.dram_tensor(name, list(shape), dt, kind="ExternalInput").ap()

    dr = {}
    dr["x"] = din("x", [NTOK, D])
    dr["ident"] = din("ident", [128, 128])
    dr["norm_g"] = din("norm_g", [2, D])
    dr["final_g"] = din("final_g", [1, D])
    dr["att_biasT"] = din("att_biasT", [128, 8, 640])
    dr["w_in_e"] = din("w_in_e", [1024, 4224])
    dr["w_out_e"] = din("w_out_e", [1024, 1024])
    dr["pcol"] = din("pcol", [128, PC_N])
    dr["lora"] = din("lora", [128, 512])
    dr["cmask"] = din("cmask", [128, 512])
    dr["sg_ln_g_c"] = din("sg_ln_g_c", [128, 8])
    dr["sg_ln_b_c"] = din("sg_ln_b_c", [128, 8])
    dr["sg_b"] = din("sg_b", [1, 1024])
    dr["sg_wT"] = din("sg_wT", [8, 128, 128])
    dr["w_in_o"] = din("w_in_o", [1024, 3072])
    dr["w_out_o"] = din("w_out_o", [1024, 1024])
    dr["yb"] = nc.dram_tensor("yb_scr", [NB * 4, 128, 4, 512], BF16, kind="Internal").ap()
    dr["h1"] = nc.dram_tensor("h1_scr", [NTOK, D], F32, kind="Internal").ap()
    dr["out"] = nc.dram_tensor("out", [NTOK, D], F32, kind="ExternalOutput").ap()
    with contextlib.ExitStack() as es_all:
        S = Sched(nc, es_all)
        with contextlib.ExitStack() as e1:
            phase1a(S, nc, e1, dr)
            S.barrier()
            S.emit()
        with contextlib.ExitStack() as e2:
            phase1b(S, nc, e2, dr)
            S.barrier()
            S.emit()
        with contextlib.ExitStack() as e3:
            outs = phase2(S, nc, e3, dr, NTOK // TT)
            S.emit(outs)
    return nc


def host_layouts(inp):
    f = lambda a: np.ascontiguousarray(np.asarray(a, dtype=np.float32))
    cols = lambda v: np.ascontiguousarray(np.asarray(v, dtype=np.float32).reshape(-1, 128).T)
    qi = np.arange(128)[:, None]
    kj = np.arange(640)[None, :]
    idx = np.clip(512 + qi - kj, -128, 128) + 128
    biasT = np.ascontiguousarray(f(inp["att_bias"])[0][:, idx].transpose(1, 0, 2))
    pcol = np.concatenate([cols(inp["shift_mu"][0]), cols(inp["rw_w0"][0]), cols(inp["rw_a0"][0]), cols(inp["rw_kk"][0]),
                           cols(inp["rw_ka"][0]), cols(np.asarray(inp["rw_rk"][0]).reshape(-1)), cols(inp["rw_lnx_g"][0]),
                           cols(inp["rw_lnx_b"][0])], 1).astype(np.float32)
    lora = np.concatenate([f(inp["rw_w2"])[0], f(inp["rw_a2"])[0]], 0)
    p = np.arange(128)[:, None] % 64
    c = np.arange(64)[None, :]
    Ma = (c > p).astype(np.float32)
    Mr = (c >= p).astype(np.float32)
    Mq = (c < p).astype(np.float32)
    I2 = (c == p).astype(np.float32)
    BDm = (np.arange(128)[:, None] // 64 == np.arange(128)[None, :] // 64).astype(np.float32)
    cmask = np.ascontiguousarray(np.concatenate([Ma, Mr, Ma, Mr, Mq, I2, BDm], 1))
    return {
        "ident": np.eye(128, dtype=np.float32),
        "norm_g": f(inp["norm_g"]),
        "final_g": f(inp["final_g"]).reshape(1, D),
        "att_biasT": biasT,
        "w_in_e": f(inp["w_in_e"])[0],
        "w_out_e": f(inp["w_out_e"])[0],
        "pcol": pcol,
        "lora": np.ascontiguousarray(lora),
        "cmask": cmask,
        "sg_ln_g_c": cols(inp["sg_ln_g"][0]),
        "sg_ln_b_c": cols(inp["sg_ln_b"][0]),
        "sg_b": f(inp["sg_b"])[0].reshape(1, 1024),
        "sg_wT": np.ascontiguousarray(f(inp["sg_w"])[0].transpose(0, 2, 1)),
        "w_in_o": f(inp["w_in_o"])[0],
        "w_out_o": f(inp["w_out_o"])[0],
    }


def kernel(**inputs):
    x = np.asarray(inputs["x"], dtype=np.float32)
    shared = host_layouts(inputs)
    nc = build_program()
    in_maps = []
    for i in range(NCORES):
        m = dict(shared)
        m["x"] = np.ascontiguousarray(x[NB * i:NB * (i + 1)].reshape(NTOK, D))
        in_maps.append(m)
    res = run_bass_kernel_spmd(nc, in_maps, core_ids=list(range(NCORES)))
    out = np.concatenate([np.asarray(r["out"], dtype=np.float32).reshape(NB, SEQ, D) for r in res.results], 0)
    return out
```
